# Optimizing a Trainium2 kernel written in Bass

```python
import math
import jax, jax.numpy as jnp
from jax import lax
import numpy as np

D_MODEL = 1024
BATCH = 4
SEQ = 8192
DEPTH = 4

HEAD_DIM = 64
N_HEADS = D_MODEL // HEAD_DIM
SB_HEADS = N_HEADS // 2
MOBA_HEADS = N_HEADS - SB_HEADS
SWA_HEADS = N_HEADS
SWA_KV_HEADS = max(1, N_HEADS // 8)
WINDOW = 128
SB_BLOCK = 128
MOBA_BLOCK = 256
MOBA_TOPK = 3
MOBA_Q_CHUNK = 32
NUM_BUCKETS = 32
MAX_DISTANCE = 4096
MEM_LEN = 256
CROSS_HEADS = 4
CROSS_HEAD_DIM = 128
D_FF = ((8 * D_MODEL // 3 + 127) // 128) * 128
CONV_WIDTH = 3
N_EVEN = (DEPTH + 1) // 2
N_ODD = DEPTH // 2
RMS_EPS = 1e-6
NEG_INF = -1e30

kernel_name = 'hybrid_stickbreak_moba_swa_trunk'


def rms_norm(x, gain):
    xf = x.astype(jnp.float32)
    y = xf * lax.rsqrt(jnp.mean(xf * xf, axis=-1, keepdims=True) + RMS_EPS)
    return (y * gain.astype(jnp.float32)).astype(x.dtype)


def t5_bucket(dist):
    n = jnp.maximum(dist, 0)
    max_exact = NUM_BUCKETS // 2
    nf = jnp.maximum(n, 1).astype(jnp.float32)
    coef = (NUM_BUCKETS - max_exact) / math.log(MAX_DISTANCE / max_exact)
    large = max_exact + (jnp.log(nf / max_exact) * coef).astype(jnp.int32)
    large = jnp.minimum(large, NUM_BUCKETS - 1)
    return jnp.where(n < max_exact, n, large)


def stick_breaking_attention(q, k, v):
    B, H, S, D = q.shape
    scale = D ** -0.5
    pos_k = jnp.arange(S)
    vf = v.astype(jnp.float32)

    def block(c):
        start = c * SB_BLOCK
        qc = lax.dynamic_slice_in_dim(q, start, SB_BLOCK, axis=2)
        z = jnp.einsum('bhqd,bhkd->bhqk', qc, k).astype(jnp.float32) * scale
        pos_q = start + jnp.arange(SB_BLOCK)
        past = pos_k[None, :] < pos_q[:, None]
        log_not = jnp.where(past, jax.nn.log_sigmoid(-z), 0.0)
        between = lax.cumsum(log_not, axis=3, reverse=True) - log_not
        w = jnp.where(past, jnp.exp(jax.nn.log_sigmoid(z) + between), 0.0)
        return jnp.einsum('bhqk,bhkd->bhqd', w, vf).astype(q.dtype)

    out = lax.map(block, jnp.arange(S // SB_BLOCK))
    return jnp.moveaxis(out, 0, 2).reshape(B, H, S, D)


def moba_attention(q, k, v, bias_table):
    B, H, S, D = q.shape
    scale = D ** -0.5
    nb = -(-S // MOBA_BLOCK)
    pad = nb * MOBA_BLOCK - S
    kb = jnp.pad(k, ((0, 0), (0, 0), (0, pad), (0, 0))).reshape(B, H, nb, MOBA_BLOCK, D)
    vb = jnp.pad(v, ((0, 0), (0, 0), (0, pad), (0, 0))).reshape(B, H, nb, MOBA_BLOCK, D)
    k_mean = jnp.mean(kb.astype(jnp.float32), axis=3)
    n_sel = max(1, min(MOBA_TOPK, nb))
    bias_hb = bias_table.T.astype(jnp.float32)
    head_idx = jnp.arange(H)[:, None, None, None]
    offs = jnp.arange(MOBA_BLOCK)
    gather = jax.vmap(jax.vmap(lambda blocks, idx: blocks[idx]))
    n_flat = n_sel * MOBA_BLOCK

    def chunk(c):
        start = c * MOBA_Q_CHUNK
        own = start // MOBA_BLOCK
        pos_q = start + jnp.arange(MOBA_Q_CHUNK)
        qc = lax.dynamic_slice_in_dim(q, start, MOBA_Q_CHUNK, axis=2)
        gate = jnp.einsum('bhqd,bhnd->bhqn', qc.astype(jnp.float32), k_mean)
        gate = jnp.where(jnp.arange(nb) < own, gate, NEG_INF)
        _, idx = lax.top_k(gate, n_sel)
        valid = jnp.arange(n_sel) < own
        kg = gather(kb, idx)
        vg = gather(vb, idx)
        dist_sel = pos_q[:, None, None] - (idx[..., None] * MOBA_BLOCK + offs)
        s_sel = (jnp.einsum('bhqd,bhqnld->bhqnl', qc, kg).astype(jnp.float32) * scale
                 + bias_hb[head_idx, t5_bucket(dist_sel)])
        s_sel = jnp.where(valid[:, None], s_sel, NEG_INF).reshape(B, H, MOBA_Q_CHUNK, n_flat)
        k_own = lax.dynamic_index_in_dim(kb, own, axis=2, keepdims=False)
        v_own = lax.dynamic_index_in_dim(vb, own, axis=2, keepdims=False)
        dist_own = pos_q[:, None] - (own * MOBA_BLOCK + offs)[None, :]
        s_own = (jnp.einsum('bhqd,bhld->bhql', qc, k_own).astype(jnp.float32) * scale
                 + bias_hb[:, t5_bucket(dist_own)])
        s_own = jnp.where(dist_own >= 0, s_own, NEG_INF)
        p = jax.nn.softmax(jnp.concatenate([s_sel, s_own], axis=-1), axis=-1)
        p_sel = p[..., :n_flat].reshape(B, H, MOBA_Q_CHUNK, n_sel, MOBA_BLOCK)
        p_own = p[..., n_flat:]
        out = (jnp.einsum('bhqnl,bhqnld->bhqd', p_sel, vg.astype(jnp.float32))
               + jnp.einsum('bhql,bhld->bhqd', p_own, v_own.astype(jnp.float32)))
        return out.astype(q.dtype)

    out = lax.map(chunk, jnp.arange(S // MOBA_Q_CHUNK))
    return jnp.moveaxis(out, 0, 2).reshape(B, H, S, D)


def swa_sink_attention(q, k, v, sinks, bias_table):
    B, S, Hq, D = q.shape
    Hkv = k.shape[2]
    G = Hq // Hkv
    blk = WINDOW
    nqb = S // blk
    scale = D ** -0.5
    qb = q.reshape(B, nqb, blk, Hkv, G, D)

    def band(t):
        tb = t.reshape(B, nqb, blk, Hkv, D)
        prev = jnp.pad(tb, ((0, 0), (1, 0), (0, 0), (0, 0), (0, 0)))[:, :-1]
        return jnp.concatenate([prev, tb], axis=2)

    kk, vv = band(k), band(v)
    qi = jnp.arange(blk)
    kj = jnp.arange(2 * blk)
    dist = qi[:, None] + blk - kj[None, :]
    in_window = (dist >= 0) & (dist < WINDOW)
    key_exists = (jnp.arange(nqb)[:, None, None] * blk - blk + kj[None, None, :]) >= 0
    mask = in_window[None] & key_exists
    bias = bias_table.astype(jnp.float32)[t5_bucket(dist)]
    bias = jnp.transpose(bias, (2, 0, 1)).reshape(Hkv, G, 1, blk, 2 * blk)
    s = jnp.einsum('bnqhgd,bnkhd->bhgnqk', qb, kk).astype(jnp.float32) * scale + bias
    s = jnp.where(mask, s, NEG_INF)
    sink = sinks.astype(jnp.float32).reshape(Hkv, G)[:, :, None, None, None]
    m = jnp.maximum(jnp.max(s, axis=-1, keepdims=True), sink)
    p = jnp.exp(s - m)
    denom = jnp.sum(p, axis=-1, keepdims=True) + jnp.exp(sink - m)
    out = jnp.einsum('bhgnqk,bnkhd->bnqhgd', p / denom, vv.astype(jnp.float32))
    return out.reshape(B, S, Hq * D).astype(q.dtype)


def even_mixer(h, w_in, w_out, q_gain, k_gain, bias_table):
    B, S, _ = h.shape
    qa, ka, va, qm, km, vm = jnp.split(h @ w_in, 6, axis=-1)

    def heads(t, n):
        return t.reshape(B, S, n, HEAD_DIM).transpose(0, 2, 1, 3)

    out_a = stick_breaking_attention(heads(qa, SB_HEADS), heads(ka, SB_HEADS), heads(va, SB_HEADS))
    qm = rms_norm(heads(qm, MOBA_HEADS), q_gain)
    km = rms_norm(heads(km, MOBA_HEADS), k_gain)
    out_b = moba_attention(qm, km, heads(vm, MOBA_HEADS), bias_table[:, SB_HEADS:])
    out = jnp.concatenate([out_a, out_b], axis=1).transpose(0, 2, 1, 3).reshape(B, S, N_HEADS * HEAD_DIM)
    return out @ w_out


def odd_mixer(h, w_in, w_out, q_gain, k_gain, sinks, bias_table):
    B, S, _ = h.shape
    q, k, v = jnp.split(h @ w_in, [SWA_HEADS * HEAD_DIM, (SWA_HEADS + SWA_KV_HEADS) * HEAD_DIM], axis=-1)
    q = rms_norm(q.reshape(B, S, SWA_HEADS, HEAD_DIM), q_gain)
    k = rms_norm(k.reshape(B, S, SWA_KV_HEADS, HEAD_DIM), k_gain)
    v = v.reshape(B, S, SWA_KV_HEADS, HEAD_DIM)
    return swa_sink_attention(q, k, v, sinks, bias_table) @ w_out


def memory_cross_attention(h, mem_n, w_q, w_kv, w_o, q_gain, k_gain):
    B, S, _ = h.shape
    M = mem_n.shape[1]
    q = rms_norm((h @ w_q).reshape(B, S, CROSS_HEADS, CROSS_HEAD_DIM), q_gain)
    k, v = jnp.split(mem_n @ w_kv, 2, axis=-1)
    k = rms_norm(k.reshape(B, M, CROSS_HEADS, CROSS_HEAD_DIM), k_gain)
    v = v.reshape(B, M, CROSS_HEADS, CROSS_HEAD_DIM)
    s = jnp.einsum('bshd,bmhd->bhsm', q, k).astype(jnp.float32) * (CROSS_HEAD_DIM ** -0.5)
    p = jax.nn.softmax(s, axis=-1)
    o = jnp.einsum('bhsm,bmhd->bshd', p, v.astype(jnp.float32)).astype(h.dtype)
    return o.reshape(B, S, CROSS_HEADS * CROSS_HEAD_DIM) @ w_o


def conv_ffn(h, w_in, conv_w, conv_b, w_out):
    S = h.shape[1]
    u = h @ w_in
    u_pad = jnp.pad(u, ((0, 0), (CONV_WIDTH - 1, 0), (0, 0)))
    c = conv_b
    for i in range(CONV_WIDTH):
        c = c + conv_w[i] * u_pad[:, i:i + S]
    gate, up = jnp.split(c, 2, axis=-1)
    return (jax.nn.silu(gate) * up) @ w_out


def setup_inputs(seed: int = 0) -> dict:
    key = jax.random.key(seed)
    ks = jax.random.split(key, 32)
    f32 = jnp.float32

    def nrm(k, shape, scale):
        return jax.random.normal(k, shape, f32) * scale

    def gain(k, shape):
        return 1.0 + 0.05 * jax.random.normal(k, shape, f32)

    D = D_MODEL
    mix_w = N_HEADS * HEAD_DIM
    ev_in = 3 * (SB_HEADS + MOBA_HEADS) * HEAD_DIM
    od_in = (SWA_HEADS + 2 * SWA_KV_HEADS) * HEAD_DIM
    cw = CROSS_HEADS * CROSS_HEAD_DIM
    return {
        'x': nrm(ks[0], (BATCH, SEQ, D), 1.0),
        'mem': nrm(ks[1], (BATCH, MEM_LEN, D), 1.0),
        'rel_bias': nrm(ks[2], (NUM_BUCKETS, N_HEADS), 0.2),
        'mix_norm': gain(ks[3], (DEPTH, D)),
        'ev_w_in': nrm(ks[4], (N_EVEN, D, ev_in), D ** -0.5),
        'ev_w_out': nrm(ks[5], (N_EVEN, mix_w, D), 0.5 * mix_w ** -0.5),
        'ev_q_gain': gain(ks[6], (N_EVEN, HEAD_DIM)),
        'ev_k_gain': gain(ks[7], (N_EVEN, HEAD_DIM)),
        'od_w_in': nrm(ks[8], (N_ODD, D, od_in), D ** -0.5),
        'od_w_out': nrm(ks[9], (N_ODD, mix_w, D), 0.5 * mix_w ** -0.5),
        'od_q_gain': gain(ks[10], (N_ODD, HEAD_DIM)),
        'od_k_gain': gain(ks[11], (N_ODD, HEAD_DIM)),
        'od_sinks': nrm(ks[12], (N_ODD, SWA_HEADS), 0.5),
        'cx_norm': gain(ks[13], (DEPTH, D)),
        'cx_mem_norm': gain(ks[14], (DEPTH, D)),
        'cx_w_q': nrm(ks[15], (DEPTH, D, cw), D ** -0.5),
        'cx_w_kv': nrm(ks[16], (DEPTH, D, 2 * cw), D ** -0.5),
        'cx_w_o': nrm(ks[17], (DEPTH, cw, D), 0.5 * cw ** -0.5),
        'cx_q_gain': gain(ks[18], (DEPTH, CROSS_HEAD_DIM)),
        'cx_k_gain': gain(ks[19], (DEPTH, CROSS_HEAD_DIM)),
        'ff_norm': gain(ks[20], (DEPTH, D)),
        'ff_w_in': nrm(ks[21], (DEPTH, D, 2 * D_FF), D ** -0.5),
        'ff_conv_w': nrm(ks[22], (DEPTH, CONV_WIDTH, 2 * D_FF), CONV_WIDTH ** -0.5),
        'ff_conv_b': nrm(ks[23], (DEPTH, 2 * D_FF), 0.02),
        'ff_w_out': nrm(ks[24], (DEPTH, D_FF, D), 0.5 * D_FF ** -0.5),
    }


def reference(x, mem, rel_bias, mix_norm, ev_w_in, ev_w_out, ev_q_gain, ev_k_gain,
              od_w_in, od_w_out, od_q_gain, od_k_gain, od_sinks,
              cx_norm, cx_mem_norm, cx_w_q, cx_w_kv, cx_w_o, cx_q_gain, cx_k_gain,
              ff_norm, ff_w_in, ff_conv_w, ff_conv_b, ff_w_out):
    for layer in range(DEPTH):
        i = layer // 2
        h = rms_norm(x, mix_norm[layer])
        if layer % 2 == 0:
            x = x + even_mixer(h, ev_w_in[i], ev_w_out[i], ev_q_gain[i], ev_k_gain[i], rel_bias)
        else:
            x = x + odd_mixer(h, od_w_in[i], od_w_out[i], od_q_gain[i], od_k_gain[i],
                              od_sinks[i], rel_bias)
        x = x + memory_cross_attention(rms_norm(x, cx_norm[layer]), rms_norm(mem, cx_mem_norm[layer]),
                                       cx_w_q[layer], cx_w_kv[layer], cx_w_o[layer],
                                       cx_q_gain[layer], cx_k_gain[layer])
        x = x + conv_ffn(rms_norm(x, ff_norm[layer]), ff_w_in[layer], ff_conv_w[layer],
                         ff_conv_b[layer], ff_w_out[layer])
    return x
```

```python
import contextlib
import math
import numpy as np
import ml_dtypes
import concourse.bass as bass
import concourse.mybir as mybir
from concourse.bass_utils import run_bass_kernel_spmd

F32 = mybir.dt.float32
BF16 = mybir.dt.bfloat16
AF = mybir.ActivationFunctionType
ALU = mybir.AluOpType
AX = mybir.AxisListType

D = 1024
DFF = 2816
NFC = 44
MEM = 256
EPS = 1e-6
NBUCK = 32
MAXD = 4096


class Buf:
    __slots__ = ("name", "w", "r")

    def __init__(self, name=""):
        self.name = name
        self.w = None
        self.r = []


class Sched:
    COMPUTE = ("pe", "act", "dve", "pool")

    def __init__(self, nc, n_dma_sems=48):
        self.nc = nc
        self.lists = {e: [] for e in ("pe", "act", "dve", "pool", "sp")}
        self.count = {e: 0 for e in self.COMPUTE}
        self.waited = {e: {} for e in self.lists}
        self.n_dma = n_dma_sems
        self.dma_use = [0] * n_dma_sems
        self.dma_next = 0
        self.ninstr = 0

    def _deps(self, eng, reads, writes, skip=None):
        need = {}

        def add(ev):
            if ev is None:
                return
            k, v = ev
            if k == skip:
                return
            if need.get(k, 0) < v:
                need[k] = v
        for b in reads:
            add(b.w)
        for b in writes:
            add(b.w)
            for ev in b.r:
                add(ev)
        waits = []
        wd = self.waited[eng]
        for k, v in need.items():
            if wd.get(k, 0) >= v:
                continue
            wd[k] = v
            waits.append((k, v))
        return waits

    def _mark(self, ev, reads, writes):
        k = ev[0]
        for b in reads:
            for i, (k2, v2) in enumerate(b.r):
                if k2 == k:
                    b.r[i] = ev
                    break
            else:
                b.r.append(ev)
        for b in writes:
            b.w = ev
            b.r = []

    def op(self, eng, fn, reads=(), writes=(), accum=False):
        waits = self._deps(eng, reads, writes, skip=(eng if accum else None))
        self.count[eng] += 1
        ev = (eng, self.count[eng])
        self.lists[eng].append((waits, fn, ev))
        self._mark(ev, reads, writes)
        self.ninstr += 1
        return ev

    def group(self, eng, fns, reads=(), writes=(), accum=False):
        waits = self._deps(eng, reads, writes, skip=(eng if accum else None))
        self.count[eng] += 1
        ev = (eng, self.count[eng])
        n = len(fns)
        for i, fn in enumerate(fns):
            self.lists[eng].append((waits if i == 0 else [], fn, ev if i == n - 1 else None))
        self._mark(ev, reads, writes)
        self.ninstr += n
        return ev

    def dma(self, eng, fn, reads=(), writes=()):
        waits = self._deps(eng, reads, writes)
        j = self.dma_next
        self.dma_next = (j + 1) % self.n_dma
        prev = self.dma_use[j]
        key = ("dma", j)
        if prev > 0 and self.waited[eng].get(key, 0) < 16 * prev:
            waits.append((key, 16 * prev))
            self.waited[eng][key] = 16 * prev
        self.dma_use[j] = prev + 1
        ev = (key, 16 * (prev + 1))
        self.lists[eng].append((waits, fn, ev))
        self._mark(ev, reads, writes)
        self.ninstr += 1
        return ev

    def barrier(self):
        evs = [(e, self.count[e]) for e in self.COMPUTE if self.count[e] > 0]
        evs += [(("dma", j), 16 * self.dma_use[j]) for j in range(self.n_dma) if self.dma_use[j] > 0]
        for eng in self.lists:
            wd = self.waited[eng]
            waits = []
            for k, v in evs:
                if wd.get(k, 0) < v:
                    wd[k] = v
                    waits.append((k, v))
            if waits:
                self.lists[eng].append((waits, None, None))

    def finalize(self):
        nc = self.nc
        self.barrier()
        with contextlib.ExitStack() as st:
            sem = {}
            for e in self.COMPUTE:
                sem[e] = st.enter_context(nc.semaphore("s_" + e))
            for j in range(self.n_dma):
                sem[("dma", j)] = st.enter_context(nc.semaphore("d%d" % j))
            block = st.enter_context(nc.Block())
            lists = self.lists

            def run(name, h):
                for waits, fn, ev in lists[name]:
                    for k, v in waits:
                        h.wait_ge(sem[k], v)
                    if fn is None:
                        continue
                    ins = fn(h)
                    if ev is not None:
                        k, v = ev
                        ins.then_inc(sem[k], 16 if isinstance(k, tuple) else 1)

            @block.tensor
            def _(h):
                run("pe", h)

            @block.scalar
            def _(h):
                run("act", h)

            @block.vector
            def _(h):
                run("dve", h)

            @block.gpsimd
            def _(h):
                run("pool", h)

            @block.sync
            def _(h):
                run("sp", h)


def MM(out, lhsT, rhs, start=True, stop=True):
    return lambda h: h.matmul(out, lhsT=lhsT, rhs=rhs, start=start, stop=stop)


def TR(out, in_, ident):
    return lambda h: h.transpose(out, in_, ident)


def ACT(out, in_, func, bias=None, scale=None):
    kw = {}
    if bias is not None:
        kw["bias"] = bias
    if scale is not None:
        kw["scale"] = scale
    return lambda h: h.activation(out=out, in_=in_, func=func, **kw)


def TT(out, in0, in1, op):
    return lambda h: h.tensor_tensor(out, in0, in1, op)


def TS(out, in0, s1, s2, op0, op1):
    return lambda h: h.tensor_scalar(out, in0, s1, s2, op0, op1)


def TS1(out, in0, s1, op0):
    return lambda h: h.tensor_single_scalar(out, in0, s1, op0)


def STT(out, in0, scalar, in1, op0, op1):
    return lambda h: h.scalar_tensor_tensor(out=out, in0=in0, scalar=scalar, in1=in1, op0=op0, op1=op1)


def CP(out, in_):
    return lambda h: h.tensor_copy(out, in_)


def MEMSET(ap, v):
    return lambda h: h.memset(ap, v)


def DMA(out, in_):
    return lambda h: h.dma_start(out=out, in_=in_)


def RECIP(out, in_):
    return lambda h: h.reciprocal(out, in_)


def VMAX(out, in_):
    return lambda h: h.max(out=out, in_=in_)


def TRED(out, in_, op):
    return lambda h: h.tensor_reduce(out, in_, AX.X, op)


class Arena:
    def __init__(self, ap, size):
        self.ap = ap
        self.size = size
        self.off = 0

    def reset(self):
        self.off = 0

    def get(self, *free):
        n = 1
        for f in free:
            n *= f
        assert self.off + n <= self.size, ("arena overflow", self.off, n, self.size)
        v = self.ap[:, self.off:self.off + n]
        self.off += n
        if len(free) == 2:
            v = v.rearrange("p (a b) -> p a b", b=free[1])
        elif len(free) == 3:
            v = v.rearrange("p (a b c) -> p a b c", b=free[1], c=free[2])
        return v


class PCols:
    def __init__(self, L):
        ne, no = (L + 1) // 2, L // 2
        self.L = L
        o = 0
        self.mixn = o; o += L * 8
        self.cxn = o; o += L * 8
        self.ffn = o; o += L * 8
        self.memn = o; o += L * 8
        self.cxqg = o; o += L
        self.cxkg = o; o += L
        self.evqg = o; o += ne
        self.evkg = o; o += ne
        self.odqg = o; o += max(no, 1)
        self.odkg = o; o += max(no, 1)
        self.sink = o; o += max(no, 1) * 16
        self.convw = o; o += L * 3 * NFC
        self.convb = o; o += L * NFC
        self.n = o


def t5_bucket_np(dist):
    n = np.maximum(dist, 0)
    max_exact = NBUCK // 2
    nf = np.maximum(n, 1).astype(np.float32)
    coef = np.float32((NBUCK - max_exact) / math.log(MAXD / max_exact))
    large = max_exact + (np.log(nf / np.float32(max_exact)) * coef).astype(np.int32)
    large = np.minimum(large, NBUCK - 1)
    return np.where(n < max_exact, n, large)


def build_program(S, L, sb_lookback=None, phases="WABMSC"):
    NT = S // 512
    NC = S // 128
    NB = S // 256
    ne, no = (L + 1) // 2, L // 2
    PC = PCols(L)
    WT = S + 384
    nc = bass.Bass("TRN2", target_bir_lowering=False)

    def din(name, shape, dt=F32):
        return nc.dram_tensor(name, list(shape), dt, kind="ExternalInput").ap()

    def dscr(name, shape, dt):
        if name in DBG.get("dump", ()):
            return nc.dram_tensor(name, list(shape), dt, kind="ExternalOutput").ap()
        return nc.dram_tensor(name, list(shape), dt).ap()

    xT_in = din("xT", [8, 128, S])
    memT_in = din("memT", [8, 128, MEM])
    params_in = din("params", [128, PC.n])
    toe_moba = din("toe_moba", [8, 128, WT])
    toe_swa = din("toe_swa", [16, 128, 256])
    cb_in = din("cb", [128, 384 + 8 * 512], BF16)
    bmask_in = din("bmask", [128, NC * NB])
    cf_in = din("cf", [128, 256])
    blkoh_in = din("blkoh", [32, S], BF16)
    w_ev_in = din("ev_w_in", [ne, D, 3072])
    w_ev_out = din("ev_w_out", [ne, D, D])
    w_od_in = din("od_w_in", [max(no, 1), D, 1280])
    w_od_out = din("od_w_out", [max(no, 1), D, D])
    w_cxq = din("cx_w_q", [L, D, 512])
    w_cxkv = din("cx_w_kv", [L, D, 1024])
    w_cxo = din("cx_w_o", [L, 512, D])
    w_ffi = din("ff_w_in", [L, D, 2 * DFF])
    w_ffo = din("ff_w_out", [L, DFF, D])
    yT = nc.dram_tensor("yT", [8, 128, S], F32, kind="ExternalOutput").ap()

    xres = dscr("xres", [8, 128, S], F32)
    b_ev_in = dscr("b_ev_in", [ne, D, 3072], BF16)
    b_ev_out = dscr("b_ev_out", [ne, D, D], BF16)
    b_od_in = dscr("b_od_in", [max(no, 1), D, 1280], BF16)
    b_od_out = dscr("b_od_out", [max(no, 1), D, D], BF16)
    b_cxq = dscr("b_cxq", [L, D, 512], BF16)
    b_cxkv = dscr("b_cxkv", [L, D, 1024], BF16)
    b_cxo = dscr("b_cxo", [L, 512, D], BF16)
    b_ffi = dscr("b_ffi", [L, NFC, 128, 8, 128], BF16)
    b_ffo = dscr("b_ffo", [L, DFF, D], BF16)
    QT = dscr("QT", [16, 64, S], BF16)
    KT = dscr("KT", [16, 64, S], BF16)
    KNT = dscr("KNT", [8, 64, S], BF16)
    Vs = dscr("Vs", [S, 1024], BF16)
    OT = dscr("OT", [1024, S], BF16)

    S_ = Sched(nc)
    op, group, dma, barrier = S_.op, S_.group, S_.dma, S_.barrier

    with contextlib.ExitStack() as st:
        def sb(name, shape, dt):
            return st.enter_context(nc.sbuf_tensor("sb_" + name, shape, dt))

        arenaF_t = sb("arenaF", [128, 16384], F32)
        arenaB_t = sb("arenaB", [128, 65536], BF16)
        AFa = Arena(arenaF_t, 16384)
        ABa = Arena(arenaB_t, 65536)
        cb = sb("cb", [128, 384 + 8 * 512], BF16)
        cf = sb("cf", [128, 256], F32)
        prm = sb("prm", [128, PC.n], F32)
        dprm = sb("dprm", [128, 64], F32)
        B_cb, B_cf, B_prm, B_dprm = Buf(), Buf(), Buf(), Buf()
        ident = cb[:, 0:128]
        ones_b = cb[:, 128:256]
        tri_b = cb[:, 256:384]
        maskZ = [cb[:, 384 + i * 512: 384 + (i + 1) * 512] for i in range(4)]
        maskG = [cb[:, 384 + (4 + i) * 512: 384 + (5 + i) * 512] for i in range(4)]
        tri_f = cf[:, 0:128]
        ones_f = cf[:, 128:256]
        PS = []
        BPS = []
        for i in range(7):
            PS.append(st.enter_context(nc.psum_tensor("ps%d" % i, [128, 512], F32)))
            BPS.append(Buf("ps%d" % i))
        PSB = st.enter_context(nc.psum_tensor("psb", [128, 1024], BF16))
        B_PSB = Buf("psb")

        dma("sp", DMA(cb[:], cb_in[:, :]), writes=[B_cb])
        dma("sp", DMA(cf[:], cf_in[:, :]), writes=[B_cf])
        dma("sp", DMA(prm[:], params_in[:, :]), writes=[B_prm])
        op("act", lambda h: h.mul(dprm[:, 0:ne], prm[:, PC.evqg:PC.evqg + ne], 0.125), reads=[B_prm], writes=[B_dprm])
        if no:
            op("act", lambda h: h.mul(dprm[:, 8:8 + no], prm[:, PC.odqg:PC.odqg + no], 0.125), reads=[B_prm, B_dprm], writes=[B_dprm])
            op("act", ACT(dprm[:, 32:32 + 16 * no], prm[:, PC.sink:PC.sink + 16 * no], AF.Exp), reads=[B_prm, B_dprm], writes=[B_dprm])
        op("act", lambda h: h.mul(dprm[:, 16:16 + L], prm[:, PC.cxqg:PC.cxqg + L], 128.0 ** -0.5), reads=[B_prm, B_dprm], writes=[B_dprm])

        def phaseW():
            AFa.reset(); ABa.reset()
            stf = [AFa.get(2048) for _ in range(3)]
            stb = [ABa.get(2048) for _ in range(3)]
            Bf = [Buf() for _ in range(3)]
            Bb = [Buf() for _ in range(3)]
            cnt = [0]
            engs = ["dve", "pool"]

            def cast2d(src, dst, R, C):
                for r0 in range(0, R, 128):
                    for c0 in range(0, C, 2048):
                        cols = min(2048, C - c0)
                        i = cnt[0] % 3
                        dma("sp", DMA(stf[i][:, 0:cols], src[r0:r0 + 128, c0:c0 + cols]), writes=[Bf[i]])
                        op(engs[cnt[0] % 2], CP(stb[i][:, 0:cols], stf[i][:, 0:cols]), reads=[Bf[i]], writes=[Bb[i]])
                        dma("sp", DMA(dst[r0:r0 + 128, c0:c0 + cols], stb[i][:, 0:cols]), reads=[Bb[i]])
                        cnt[0] += 1

            for l in range(L):
                i = l // 2
                if l % 2 == 0:
                    cast2d(w_ev_in[i], b_ev_in[i], D, 3072)
                    cast2d(w_ev_out[i], b_ev_out[i], D, D)
                else:
                    cast2d(w_od_in[i], b_od_in[i], D, 1280)
                    cast2d(w_od_out[i], b_od_out[i], D, D)
                cast2d(w_cxq[l], b_cxq[l], D, 512)
                cast2d(w_cxkv[l], b_cxkv[l], D, 1024)
                cast2d(w_cxo[l], b_cxo[l], 512, D)
                cast2d(w_ffo[l], b_ffo[l], DFF, D)
                for kc in range(8):
                    for c0 in range(0, 2 * DFF, 2048):
                        cols = min(2048, 2 * DFF - c0)
                        nch = cols // 128
                        ch0 = c0 // 128
                        i3 = cnt[0] % 3
                        dma("sp", DMA(stf[i3][:, 0:cols], w_ffi[l][kc * 128:(kc + 1) * 128, c0:c0 + cols]), writes=[Bf[i3]])
                        op(engs[cnt[0] % 2], CP(stb[i3][:, 0:cols], stf[i3][:, 0:cols]), reads=[Bf[i3]], writes=[Bb[i3]])
                        dst = b_ffi[l][ch0:ch0 + nch, :, kc, :].rearrange("c p n -> p c n")
                        dma("sp", DMA(dst, stb[i3][:, 0:cols].rearrange("p (c n) -> p c n", n=128)), reads=[Bb[i3]])
                        cnt[0] += 1

        def rmsnorm(xt, Bx, gcol0, hT, Bh, sq, Bsq, lnv, Blnv, rstd, Brstd, pbank, N=512):
            op("act", ACT(sq[:, :, 0:N], xt[:, :, 0:N], AF.Square), reads=[Bx], writes=[Bsq])
            group("pe", [MM(PS[pbank][:, 0:N], ones_b, sq[:, c, 0:N], start=(c == 0), stop=(c == 7)) for c in range(8)],
                  reads=[Bsq, B_cb], writes=[BPS[pbank]])
            op("act", ACT(lnv[:, 0:N], PS[pbank][:, 0:N], AF.Ln, bias=EPS, scale=1.0 / D), reads=[BPS[pbank]], writes=[Blnv])
            op("act", ACT(rstd[:, 0:N], lnv[:, 0:N], AF.Exp, scale=-0.5), reads=[Blnv], writes=[Brstd])
            for c in range(8):
                op("dve", STT(hT[:, c, 0:N], xt[:, c, 0:N], prm[:, gcol0 + c:gcol0 + c + 1], rstd[:, 0:N], ALU.mult, ALU.mult),
                   reads=[Bx, Brstd, B_prm], writes=[Bh])

        def headnorm(psrc, Bsrc, P, dimscale, gcol, out, Bout, sqh, Bsqh, lnv, Blnv, rstd, Brstd, pbank, N=512):
            op("act", ACT(sqh[0:P, 0:N], psrc, AF.Square), reads=[Bsrc], writes=[Bsqh])
            op("pe", MM(PS[pbank][0:P, 0:N], ones_b[0:P, 0:P], sqh[0:P, 0:N]), reads=[Bsqh, B_cb], writes=[BPS[pbank]])
            op("act", ACT(lnv[0:P, 0:N], PS[pbank][0:P, 0:N], AF.Ln, bias=EPS, scale=dimscale), reads=[BPS[pbank]], writes=[Blnv])
            op("act", ACT(rstd[0:P, 0:N], lnv[0:P, 0:N], AF.Exp, scale=-0.5), reads=[Blnv], writes=[Brstd])
            op("dve", STT(out, psrc, gcol, rstd[0:P, 0:N], ALU.mult, ALU.mult), reads=[Bsrc, Brstd, B_prm, B_dprm], writes=[Bout])

        def phaseA(l):
            even = (l % 2 == 0)
            li = l // 2
            AFa.reset(); ABa.reset()
            ncol = 3072 if even else 1280
            wsb = ABa.get(8, ncol); Bw = Buf()
            wsrc = (b_ev_in if even else b_od_in)[li]
            for kc in range(8):
                dma("sp", DMA(wsb[:, kc, :], wsrc[kc * 128:(kc + 1) * 128, :]), writes=[Bw])
            xt = [AFa.get(8, 512) for _ in range(2)]; Bx = [Buf(), Buf()]
            hT = [ABa.get(8, 512) for _ in range(2)]; Bh = [Buf(), Buf()]
            sq = ABa.get(8, 512); Bsq = Buf()
            lnv = AFa.get(512); Blnv = Buf()
            rstd = AFa.get(512); Brstd = Buf()
            sqh = [ABa.get(512) for _ in range(2)]; Bsqh = [Buf(), Buf()]
            lnh = [AFa.get(512) for _ in range(2)]; Blnh = [Buf(), Buf()]
            rsh = [AFa.get(512) for _ in range(2)]; Brsh = [Buf(), Buf()]
            NSTG = 6
            stg = [ABa.get(512) for _ in range(NSTG)]; Bstg = [Buf() for _ in range(NSTG)]
            vst = [ABa.get(1024) for _ in range(2)]; Bvst = [Buf(), Buf()]
            src = xT_in if l == 0 else xres
            sc = [0]
            hc = [0]

            def nstg():
                i = sc[0] % NSTG
                sc[0] += 1
                return stg[i], Bstg[i]

            for n in range(NT):
                b = n % 2
                ts = slice(n * 512, (n + 1) * 512)
                dma("sp", DMA(xt[b][:], src[:, :, ts].rearrange("c p t -> p c t")), writes=[Bx[b]])
                rmsnorm(xt[b], Bx[b], PC.mixn + l * 8, hT[b], Bh[b], sq, Bsq, lnv, Blnv, rstd, Brstd, 6)
                h = hT[b]
                if DBG.get("A") == "norm":
                    continue

                def proj64(col0, pb):
                    group("pe", [MM(PS[pb][0:64, :], wsb[:, kc, col0:col0 + 64], h[:, kc, :], start=(kc == 0), stop=(kc == 7)) for kc in range(8)],
                          reads=[Bw, Bh[b]], writes=[BPS[pb]])

                pbc = [0]

                def nextpb():
                    pb = pbc[0] % 4
                    pbc[0] += 1
                    return pb

                def normed(pb, gcol, dst):
                    k = hc[0] % 2
                    hc[0] += 1
                    sg, Bsg = nstg()
                    headnorm(PS[pb][0:64, :], BPS[pb], 64, 1.0 / 64, gcol, sg[0:64, :], Bsg,
                             sqh[k], Bsqh[k], lnh[k], Blnh[k], rsh[k], Brsh[k], 4 + k)
                    dma("sp", DMA(dst, sg[0:64, :]), reads=[Bsg])

                if even:
                    for hd in range(16):
                        if DBG.get("A") == "sb" and hd >= 8:
                            continue
                        if DBG.get("A") == "v":
                            continue
                        grp = hd // 8
                        hh = hd % 8
                        qcol = grp * 1536 + hh * 64
                        kcol = grp * 1536 + 512 + hh * 64
                        pb = nextpb()
                        proj64(qcol, pb)
                        if grp == 0:
                            sg, Bsg = nstg()
                            op("act", ACT(sg[0:64, :], PS[pb][0:64, :], AF.Copy), reads=[BPS[pb]], writes=[Bsg])
                            if not DBG.get("nostore"):
                                dma("sp", DMA(QT[hd][:, ts], sg[0:64, :]), reads=[Bsg])
                            if DBG.get("qonly"):
                                continue
                        else:
                            normed(pb, dprm[0:64, li:li + 1], QT[hd][:, ts])
                        pb = nextpb()
                        proj64(kcol, pb)
                        if grp == 0:
                            sg, Bsg = nstg()
                            op("act", ACT(sg[0:64, :], PS[pb][0:64, :], AF.Copy), reads=[BPS[pb]], writes=[Bsg])
                            dma("sp", DMA(KT[hd][:, ts], sg[0:64, :]), reads=[Bsg])
                            if DBG.get("nokn"):
                                continue
                            sg2, Bsg2 = nstg()
                            op("act", ACT(sg2[0:64, :], PS[pb][0:64, :], AF.Copy, scale=-0.125), reads=[BPS[pb]], writes=[Bsg2])
                            if not DBG.get("noknstore"):
                                dma("sp", DMA(KNT[hd][:, ts], sg2[0:64, :]), reads=[Bsg2])
                        else:
                            normed(pb, prm[0:64, PC.evkg + li:PC.evkg + li + 1], KT[hd][:, ts])
                    for tc in range(4):
                        vb = tc % 2
                        for half in range(2):
                            pb = nextpb()
                            vcol = 1024 + half * 1536
                            group("pe", [MM(PS[pb][:, :], h[:, kc, tc * 128:(tc + 1) * 128], wsb[:, kc, vcol:vcol + 512], start=(kc == 0), stop=(kc == 7)) for kc in range(8)],
                                  reads=[Bw, Bh[b]], writes=[BPS[pb]])
                            op("dve", CP(vst[vb][:, half * 512:(half + 1) * 512], PS[pb][:, :]), reads=[BPS[pb]], writes=[Bvst[vb]])
                        r0 = n * 512 + tc * 128
                        dma("sp", DMA(Vs[r0:r0 + 128, :], vst[vb][:, :]), reads=[Bvst[vb]])
                else:
                    for hd in range(16):
                        pb = nextpb()
                        proj64(hd * 64, pb)
                        normed(pb, dprm[0:64, 8 + li:8 + li + 1], QT[hd][:, ts])
                    for kh in range(2):
                        pb = nextpb()
                        proj64(1024 + kh * 64, pb)
                        normed(pb, prm[0:64, PC.odkg + li:PC.odkg + li + 1], KT[kh][:, ts])
                    for tc in range(4):
                        vb = tc % 2
                        pb = nextpb()
                        group("pe", [MM(PS[pb][:, 0:128], h[:, kc, tc * 128:(tc + 1) * 128], wsb[:, kc, 1152:1280], start=(kc == 0), stop=(kc == 7)) for kc in range(8)],
                              reads=[Bw, Bh[b]], writes=[BPS[pb]])
                        op("dve", CP(vst[vb][:, 0:128], PS[pb][:, 0:128]), reads=[BPS[pb]], writes=[Bvst[vb]])
                        r0 = n * 512 + tc * 128
                        dma("sp", DMA(Vs[r0:r0 + 128, 0:128], vst[vb][:, 0:128]), reads=[Bvst[vb]])

        def phaseB_sb():
            AFa.reset(); ABa.reset()
            QQ = ABa.get(S); BQQ = Buf()
            KK = ABa.get(S); BKK = Buf()
            Vh = ABa.get(NC, 64); BVh = Buf()
            wsb_ = [ABa.get(512) for _ in range(3)]; Bw_ = [Buf() for _ in range(3)]
            ostg = [ABa.get(512) for _ in range(2)]; Bos = [Buf(), Buf()]
            L_sb = [ABa.get(512) for _ in range(4)]; BL = [Buf() for _ in range(4)]
            R_sb = [ABa.get(512) for _ in range(3)]; BR = [Buf() for _ in range(3)]
            e_sb = [AFa.get(512) for _ in range(2)]; Be = [Buf(), Buf()]
            it = 0
            for hd in range(8):
                dma("sp", DMA(QQ[0:64, :], QT[hd][:, :]), writes=[BQQ])
                dma("sp", DMA(QQ[64:128, :], QT[hd][:, :]), writes=[BQQ])
                dma("sp", DMA(KK[0:64, :], KT[hd][:, :]), writes=[BKK])
                dma("sp", DMA(KK[64:128, :], KNT[hd][:, :]), writes=[BKK])
                dma("sp", DMA(Vh[:, :, :], Vs[:, hd * 64:(hd + 1) * 64].rearrange("(c p) d -> p c d", p=128)), writes=[BVh])
                tiles = []
                for g in range(NT):
                    kc_hi = 4 * g + 3
                    kc_lo = 0 if sb_lookback is None else max(0, 4 * g - sb_lookback)
                    chunks = list(range(kc_hi, kc_lo - 1, -1))
                    for j, kc in enumerate(chunks):
                        tiles.append((g, j, kc, len(chunks), it))
                        it += 1

                def stage1(t):
                    g, j, kc, n, itx = t
                    diag = kc >= 4 * g
                    i = kc - 4 * g
                    ks = slice(kc * 128, (kc + 1) * 128)
                    qs = slice(g * 512, (g + 1) * 512)
                    PZ = itx % 2
                    eb = itx % 2
                    lb = itx % 4
                    fns = [MM(PS[PZ][:, :], KK[0:64, ks], QQ[0:64, qs], start=True, stop=not diag)]
                    if diag:
                        fns.append(MM(PS[PZ][:, :], ident, maskZ[i], start=False, stop=True))
                    group("pe", fns, reads=[BKK, BQQ, B_cb], writes=[BPS[PZ]])
                    op("act", ACT(e_sb[eb][:, :], PS[PZ][:, :], AF.Exp, scale=0.125), reads=[BPS[PZ]], writes=[Be[eb]])
                    op("act", ACT(L_sb[lb][:, :], e_sb[eb][:, :], AF.Ln, bias=1.0), reads=[Be[eb]], writes=[BL[lb]])

                state = {"R": None, "BR": None}

                def stage2(t):
                    g, j, kc, n, itx = t
                    diag = kc >= 4 * g
                    i = kc - 4 * g
                    ks = slice(kc * 128, (kc + 1) * 128)
                    qs = slice(g * 512, (g + 1) * 512)
                    PG = 2 + (itx % 2)
                    lb = itx % 4
                    if j == 0:
                        state["R"] = None
                        state["BR"] = None
                    Rprev, BRprev = state["R"], state["BR"]
                    fns = [MM(PS[PG][:, :], KK[64:128, ks], QQ[64:128, qs], start=True, stop=False),
                           MM(PS[PG][:, :], tri_b, L_sb[lb][:, :], start=False, stop=(Rprev is None and not diag))]
                    rd = [BKK, BQQ, B_cb, BL[lb]]
                    if Rprev is not None:
                        fns.append(MM(PS[PG][:, :], ones_b, Rprev, start=False, stop=not diag))
                        rd.append(BRprev)
                    if diag:
                        fns.append(MM(PS[PG][:, :], ident, maskG[i], start=False, stop=True))
                    group("pe", fns, reads=rd, writes=[BPS[PG]])
                    if j + 1 < n:
                        if Rprev is None:
                            state["R"], state["BR"] = L_sb[lb], BL[lb]
                        else:
                            rb = itx % 3
                            op("dve", TT(R_sb[rb][:, :], Rprev[:, :], L_sb[lb][:, :], ALU.add), reads=[BRprev, BL[lb]], writes=[BR[rb]])
                            state["R"], state["BR"] = R_sb[rb], BR[rb]
                    wb = itx % 3
                    op("act", ACT(wsb_[wb][:, :], PS[PG][:, :], AF.Exp, scale=-1.0), reads=[BPS[PG]], writes=[Bw_[wb]])

                def stage3(t):
                    g, j, kc, n, itx = t
                    qs = slice(g * 512, (g + 1) * 512)
                    PO = 4 + (g % 2)
                    wb = itx % 3
                    op("pe", MM(PS[PO][0:64, :], Vh[:, kc, :], wsb_[wb][:, :], start=(j == 0), stop=(j == n - 1)),
                       reads=[BVh, Bw_[wb]], writes=[BPS[PO]], accum=(j > 0))
                    if j == n - 1:
                        ob = g % 2
                        op("dve", CP(ostg[ob][0:64, :], PS[PO][0:64, :]), reads=[BPS[PO]], writes=[Bos[ob]])
                        dma("sp", DMA(OT[hd * 64:(hd + 1) * 64, qs], ostg[ob][0:64, :]), reads=[Bos[ob]])

                nt_ = len(tiles)
                stage1(tiles[0])
                if nt_ > 1:
                    stage1(tiles[1])
                stage2(tiles[0])
                for ti in range(nt_):
                    if ti + 2 < nt_:
                        stage1(tiles[ti + 2])
                    if ti + 1 < nt_:
                        stage2(tiles[ti + 1])
                    stage3(tiles[ti])

        def phaseB_moba():
            AFa.reset(); ABa.reset()
            NBc = NB
            Qa = ABa.get(S); BQa = Buf()
            Ka = ABa.get(S); BKa = Buf()
            Vh = ABa.get(NC, 64); BVh = Buf()
            E = ABa.get(WT); BE = Buf()
            psb = [ABa.get(512) for _ in range(3)]; Bp = [Buf() for _ in range(3)]
            pe_ = [ABa.get(512) for _ in range(3)]; Bpe = [Buf() for _ in range(3)]
            Mw8 = [ABa.get(8, 96) for _ in range(2)]; BMw = [Buf(), Buf()]
            ostg = [ABa.get(512) for _ in range(2)]; Bos = [Buf(), Buf()]
            Qf = AFa.get(S); BQf = Buf()
            Est = [AFa.get(1024) for _ in range(2)]; BEst = [Buf(), Buf()]
            km = AFa.get(NBc); Bkm = Buf()
            gm = [AFa.get(8 * NBc) for _ in range(2)]; Bgm = [Buf(), Buf()]
            top8 = [AFa.get(8) for _ in range(4)]; Bt8 = [Buf() for _ in range(4)]
            rd_ = [AFa.get(512) for _ in range(2)]; Brd = [Buf(), Buf()]
            bmk = AFa.get(NC * NBc); Bbmk = Buf()
            dma("sp", DMA(bmk[:, :], bmask_in[:, :]), writes=[Bbmk])
            dma("sp", DMA(Ka[64:96, :], blkoh_in[:, :]), writes=[BKa])
            for k in range(2):
                op("pool", MEMSET(Mw8[k][:, :, :], 0.0), writes=[BMw[k]])
            it = 0
            tcnt = 0
            for hm in range(8):
                hd = 8 + hm
                dma("sp", DMA(Qa[0:64, :], QT[hd][:, :]), writes=[BQa])
                dma("sp", DMA(Ka[0:64, :], KT[hd][:, :]), writes=[BKa])
                dma("sp", DMA(Vh[:, :, :], Vs[:, hd * 64:(hd + 1) * 64].rearrange("(c p) d -> p c d", p=128)), writes=[BVh])
                for ci, c0 in enumerate(range(0, WT, 1024)):
                    cols = min(1024, WT - c0)
                    k = ci % 2
                    dma("sp", DMA(Est[k][:, 0:cols], toe_moba[hm][:, c0:c0 + cols]), writes=[BEst[k]])
                    op("act", ACT(E[:, c0:c0 + cols], Est[k][:, 0:cols], AF.Exp), reads=[BEst[k]], writes=[BE])
                for c0 in range(0, S, 1024):
                    op("pool", CP(Qf[0:64, c0:c0 + 1024], Qa[0:64, c0:c0 + 1024]), reads=[BQa], writes=[BQf])
                op("dve", TRED(km[0:64, :], Ka[0:64, :].rearrange("p (b k) -> p b k", k=256), ALU.add), reads=[BKa], writes=[Bkm])
                for b0 in range(0, NC, 8):
                    k = (b0 // 8) % 2
                    group("pe", [MM(PS[6][:, t * NBc:(t + 1) * NBc], Qf[0:64, (b0 + t) * 128:(b0 + t + 1) * 128], km[0:64, 0:NBc]) for t in range(8)],
                          reads=[BQf, Bkm], writes=[BPS[6]])
                    op("dve", TT(gm[k][:, :], PS[6][:, 0:8 * NBc], bmk[:, b0 * NBc:(b0 + 8) * NBc], ALU.add), reads=[BPS[6], Bbmk], writes=[Bgm[k]])
                    for t in range(8):
                        k4 = tcnt % 4
                        tcnt += 1
                        op("dve", VMAX(top8[k4][:, :], gm[k][:, t * NBc:(t + 1) * NBc]), reads=[Bgm[k]], writes=[Bt8[k4]])
                        op("dve", TS(Mw8[k][:, t, 64:64 + NBc], gm[k][:, t * NBc:(t + 1) * NBc], top8[k4][:, 3:4], 128.0, ALU.is_ge, ALU.mult),
                           reads=[Bgm[k], Bt8[k4]], writes=[BMw[k]])
                    group("pe", [TR(PSB[0:96, t * 128:(t + 1) * 128], Mw8[k][:, t, 0:96], ident) for t in range(8)],
                          reads=[BMw[k], B_cb], writes=[B_PSB])
                    op("act", ACT(Qa[64:96, b0 * 128:(b0 + 8) * 128], PSB[64:96, 0:1024], AF.Copy), reads=[B_PSB], writes=[BQa])
                tiles = []
                for g in range(NT):
                    nchk = 4 * g + 4
                    for j in range(nchk):
                        tiles.append((g, j, j, nchk, it))
                        it += 1

                def stage1(t):
                    g, j, kc, n, itx = t
                    ks = slice(kc * 128, (kc + 1) * 128)
                    qs = slice(g * 512, (g + 1) * 512)
                    pz = itx % 3
                    op("pe", MM(PS[pz][:, :], Ka[0:96, ks], Qa[0:96, qs]), reads=[BKa, BQa], writes=[BPS[pz]])
                    op("act", ACT(psb[pz][:, :], PS[pz][:, :], AF.Exp, bias=-128.0), reads=[BPS[pz]], writes=[Bp[pz]])
                    c0 = 512 * g - 128 * kc + 384
                    op("dve", TT(pe_[pz][:, :], psb[pz][:, :], E[:, c0:c0 + 512], ALU.mult), reads=[Bp[pz], BE], writes=[Bpe[pz]])

                def stage2(t):
                    g, j, kc, n, itx = t
                    qs = slice(g * 512, (g + 1) * 512)
                    pz = itx % 3
                    PN = 3 + (g % 2)
                    PD = 5 + (g % 2)
                    op("pe", MM(PS[PN][0:64, :], Vh[:, kc, :], pe_[pz][:, :], start=(j == 0), stop=(j == n - 1)),
                       reads=[BVh, Bpe[pz]], writes=[BPS[PN]], accum=(j > 0))
                    op("pe", MM(PS[PD][0:64, :], ones_b[:, 0:64], pe_[pz][:, :], start=(j == 0), stop=(j == n - 1)),
                       reads=[B_cb, Bpe[pz]], writes=[BPS[PD]], accum=(j > 0))
                    if j == n - 1:
                        ob = g % 2
                        op("dve", RECIP(rd_[ob][0:64, :], PS[PD][0:64, :]), reads=[BPS[PD]], writes=[Brd[ob]])
                        op("dve", TT(ostg[ob][0:64, :], PS[PN][0:64, :], rd_[ob][0:64, :], ALU.mult), reads=[BPS[PN], Brd[ob]], writes=[Bos[ob]])
                        dma("sp", DMA(OT[hd * 64:(hd + 1) * 64, qs], ostg[ob][0:64, :]), reads=[Bos[ob]])

                stage1(tiles[0])
                if len(tiles) > 1:
                    stage1(tiles[1])
                for ti, t in enumerate(tiles):
                    if ti + 2 < len(tiles):
                        stage1(tiles[ti + 2])
                    stage2(t)

        def phaseB_swa(li):
            AFa.reset(); ABa.reset()
            Q4 = ABa.get(4, S); BQ4 = Buf()
            Ks = ABa.get(S); BKs = Buf()
            Vh = ABa.get(NC, 64); BVh = Buf()
            Esw = ABa.get(16, 2, 128); BEsw = Buf()
            psb = [ABa.get(512) for _ in range(3)]; Bp = [Buf() for _ in range(3)]
            pe_ = [ABa.get(512) for _ in range(3)]; Bpe = [Buf() for _ in range(3)]
            ostg = [ABa.get(4, 128) for _ in range(2)]; Bos = [Buf(), Buf()]
            Est = [AFa.get(256) for _ in range(2)]; BEst = [Buf(), Buf()]
            den = [AFa.get(4, 128) for _ in range(2)]; Bden = [Buf(), Buf()]
            for hh in range(16):
                k = hh % 2
                dma("sp", DMA(Est[k][:, :], toe_swa[hh][:, :]), writes=[BEst[k]])
                op("act", ACT(Esw[:, hh, :, :], Est[k][:, :].rearrange("p (a b) -> p a b", b=128), AF.Exp), reads=[BEst[k]], writes=[BEsw])
            OTv = OT.rearrange("(h d) t -> d h t", d=64)
            it = 0
            for kv in range(2):
                dma("sp", DMA(Ks[0:64, :], KT[kv][:, :]), writes=[BKs])
                dma("sp", DMA(Vh[:, :, :], Vs[:, kv * 64:(kv + 1) * 64].rearrange("(c p) d -> p c d", p=128)), writes=[BVh])
                for hg in range(2):
                    h0 = kv * 8 + hg * 4
                    for jj in range(4):
                        dma("sp", DMA(Q4[0:64, jj, :], QT[h0 + jj][:, :]), writes=[BQ4])
                    tiles = []
                    for qb in range(NC):
                        cks = [(qb - 1, 0), (qb, 1)] if qb > 0 else [(qb, 1)]
                        for j, (ck, which) in enumerate(cks):
                            tiles.append((qb, j, ck, which, len(cks), it))
                            it += 1

                    def stage1(t, h0=h0):
                        qb, j, ck, which, n, itx = t
                        qs = slice(qb * 128, (qb + 1) * 128)
                        ks = slice(ck * 128, (ck + 1) * 128)
                        pz = itx % 3
                        op("pe", MM(PS[pz][:, :], Ks[0:64, ks], Q4[0:64, :, qs]), reads=[BKs, BQ4], writes=[BPS[pz]])
                        op("act", ACT(psb[pz][:, :], PS[pz][:, :], AF.Exp), reads=[BPS[pz]], writes=[Bp[pz]])
                        op("dve", TT(pe_[pz][:, :].rearrange("p (a b) -> p a b", b=128), psb[pz][:, :].rearrange("p (a b) -> p a b", b=128),
                                     Esw[:, h0:h0 + 4, which, :], ALU.mult), reads=[Bp[pz], BEsw], writes=[Bpe[pz]])

                    def stage2(t, h0=h0):
                        qb, j, ck, which, n, itx = t
                        qs = slice(qb * 128, (qb + 1) * 128)
                        pz = itx % 3
                        PN = 3 + (qb % 2)
                        PD = 5 + (qb % 2)
                        last = (j == n - 1)
                        op("pe", MM(PS[PN][0:64, :], Vh[:, ck, :], pe_[pz][:, :], start=(j == 0), stop=last),
                           reads=[BVh, Bpe[pz]], writes=[BPS[PN]], accum=(j > 0))
                        op("pe", MM(PS[PD][0:64, :], ones_b[:, 0:64], pe_[pz][:, :], start=(j == 0), stop=last),
                           reads=[B_cb, Bpe[pz]], writes=[BPS[PD]], accum=(j > 0))
                        if last:
                            ob = qb % 2
                            sk = dprm[0:64, 32 + li * 16 + h0:32 + li * 16 + h0 + 4].unsqueeze(2).to_broadcast([64, 4, 128])
                            op("dve", TT(den[ob][0:64, :, :], PS[PD][0:64, :].rearrange("p (a b) -> p a b", b=128), sk, ALU.add),
                               reads=[BPS[PD], B_dprm], writes=[Bden[ob]])
                            op("dve", RECIP(den[ob][0:64, :, :], den[ob][0:64, :, :]), reads=[Bden[ob]], writes=[Bden[ob]])
                            op("dve", TT(ostg[ob][0:64, :, :], PS[PN][0:64, :].rearrange("p (a b) -> p a b", b=128), den[ob][0:64, :, :], ALU.mult),
                               reads=[BPS[PN], Bden[ob]], writes=[Bos[ob]])
                            dma("sp", DMA(OTv[:, h0:h0 + 4, qs], ostg[ob][0:64, :, :]), reads=[Bos[ob]])

                    stage1(tiles[0])
                    if len(tiles) > 1:
                        stage1(tiles[1])
                    for ti, t in enumerate(tiles):
                        if ti + 2 < len(tiles):
                            stage1(tiles[ti + 2])
                        stage2(t)

        def phaseC(l, last):
            even = (l % 2 == 0)
            li = l // 2
            AFa.reset(); ABa.reset()
            wout = ABa.get(8, D); Bwout = Buf()
            wq = ABa.get(8, 512); Bwq = Buf()
            wo = ABa.get(4, D); Bwo = Buf()
            KmT = ABa.get(4, MEM); BKm = Buf()
            Vm = ABa.get(2, 512); BVm = Buf()
            ot = ABa.get(8, 512); Bot = Buf()
            hT = [ABa.get(8, 512) for _ in range(2)]; Bh = [Buf(), Buf()]
            sq = ABa.get(8, 512); Bsq = Buf()
            qn = ABa.get(512); Bqn = Buf()
            pc = [ABa.get(512) for _ in range(2)]; Bpc = [Buf(), Buf()]
            o2 = ABa.get(4, 512); Bo2 = Buf()
            aT = ABa.get(22, 512); BaT = [Buf() for _ in range(22)]
            NW1 = 4
            wch = [ABa.get(8, 128) for _ in range(NW1)]; Bwch = [Buf() for _ in range(NW1)]
            NW2 = 4
            wo2 = [ABa.get(512) for _ in range(NW2)]; Bwo2 = [Buf() for _ in range(NW2)]
            sqh = ABa.get(512); Bsqh = Buf()
            xt = [AFa.get(8, 512) for _ in range(2)]; Bx = [Buf(), Buf()]
            lnv = AFa.get(512); Blnv = Buf()
            rstd = AFa.get(512); Brstd = Buf()
            rdc = AFa.get(512); Brdc = Buf()
            U = [AFa.get(514) for _ in range(3)]; BU = [Buf() for _ in range(3)]
            c1 = [AFa.get(512) for _ in range(2)]; Bc1 = [Buf(), Buf()]
            cg = AFa.get(512); Bcg = Buf()
            cu = AFa.get(512); Bcu = Buf()
            sgl = AFa.get(512); Bsgl = Buf()
            hist = AFa.get(NFC, 2); Bhist = [Buf() for _ in range(NFC)]
            wosrc = (b_ev_out if even else b_od_out)[li]
            dma("sp", DMA(wout[:, :, :], wosrc.rearrange("(c p) f -> p c f", p=128)), writes=[Bwout])
            dma("sp", DMA(wq[:, :, :], b_cxq[l].rearrange("(c p) f -> p c f", p=128)), writes=[Bwq])
            dma("sp", DMA(wo[:, :, :], b_cxo[l].rearrange("(c p) f -> p c f", p=128)), writes=[Bwo])
            wkv = aT[:, 0:16, :].rearrange("p a b -> p (a b)").rearrange("p (a b) -> p a b", b=1024)
            Bwkv = Buf()
            dma("sp", DMA(wkv, b_cxkv[l].rearrange("(c p) f -> p c f", p=128)), writes=[Bwkv])
            mx = xt[0][:, :, 0:MEM]
            dma("sp", DMA(mx, memT_in.rearrange("c p t -> p c t")), writes=[Bx[0]])
            rmsnorm(xt[0], Bx[0], PC.memn + l * 8, hT[1], Bh[1], sq, Bsq, lnv, Blnv, rstd, Brstd, 6, N=MEM)
            memn = hT[1]
            for j in range(4):
                group("pe", [MM(PS[0][:, 0:MEM], wkv[:, kc, j * 128:(j + 1) * 128], memn[:, kc, 0:MEM], start=(kc == 0), stop=(kc == 7)) for kc in range(8)],
                      reads=[Bwkv, Bh[1]], writes=[BPS[0]])
                headnorm(PS[0][:, 0:MEM], BPS[0], 128, 1.0 / 128, prm[:, PC.cxkg + l:PC.cxkg + l + 1], KmT[:, j, :], BKm,
                         sqh, Bsqh, lnv, Blnv, rstd, Brstd, 1, N=MEM)
            for mc in range(2):
                group("pe", [MM(PS[2][:, :], memn[:, kc, mc * 128:(mc + 1) * 128], wkv[:, kc, 512:1024], start=(kc == 0), stop=(kc == 7)) for kc in range(8)],
                      reads=[Bwkv, Bh[1]], writes=[BPS[2]])
                op("dve", CP(Vm[:, mc, :], PS[2][:, :]), reads=[BPS[2]], writes=[BVm])
            for b_ in BaT:
                b_.r = list(Bwkv.r); b_.w = Bwkv.w
            for c in range(NFC):
                op("pool", MEMSET(hist[:, c, :], 0.0), writes=[Bhist[c]])
            xsrc = xT_in if l == 0 else xres
            xdst = yT if last else xres
            cnts = {"w1": 0, "w2": 0, "uc": 0}

            def prologue(n):
                b = n % 2
                ts = slice(n * 512, (n + 1) * 512)
                x = xt[b]
                dma("sp", DMA(x[:], xsrc[:, :, ts].rearrange("c p t -> p c t")), writes=[Bx[b]])
                dma("sp", DMA(ot[:, :, :], OT[:, ts].rearrange("(c p) t -> p c t", p=128)), writes=[Bot])
                yield
                for fo in range(8):
                    pb = fo % 2
                    group("pe", [MM(PS[pb][:, :], wout[:, c, fo * 128:(fo + 1) * 128], ot[:, c, :], start=(c == 0), stop=(c == 7)) for c in range(8)],
                          reads=[Bwout, Bot], writes=[BPS[pb]])
                    op("dve", TT(x[:, fo, :], x[:, fo, :], PS[pb][:, :], ALU.add), reads=[BPS[pb], Bx[b]], writes=[Bx[b]])
                    yield
                rmsnorm(x, Bx[b], PC.cxn + l * 8, hT[0], Bh[0], sq, Bsq, lnv, Blnv, rstd, Brstd, 3)
                yield
                for j in range(4):
                    group("pe", [MM(PS[2][:, :], wq[:, kc, j * 128:(j + 1) * 128], hT[0][:, kc, :], start=(kc == 0), stop=(kc == 7)) for kc in range(8)],
                          reads=[Bwq, Bh[0]], writes=[BPS[2]])
                    yield
                    headnorm(PS[2][:, :], BPS[2], 128, 1.0 / 128, dprm[:, 16 + l:17 + l], qn[:, :], Bqn,
                             sqh, Bsqh, lnv, Blnv, rstd, Brstd, 3)
                    yield
                    for mc in range(2):
                        op("pe", MM(PS[mc][:, :], KmT[:, j, mc * 128:(mc + 1) * 128], qn[:, :]), reads=[BKm, Bqn], writes=[BPS[mc]])
                        op("act", ACT(pc[mc][:, :], PS[mc][:, :], AF.Exp), reads=[BPS[mc]], writes=[Bpc[mc]])
                    yield
                    group("pe", [MM(PS[3][:, :], Vm[:, mc, j * 128:(j + 1) * 128], pc[mc][:, :], start=(mc == 0), stop=(mc == 1)) for mc in range(2)],
                          reads=[BVm, Bpc[0], Bpc[1]], writes=[BPS[3]])
                    group("pe", [MM(PS[2][:, :], ones_b, pc[mc][:, :], start=(mc == 0), stop=(mc == 1)) for mc in range(2)],
                          reads=[B_cb, Bpc[0], Bpc[1]], writes=[BPS[2]])
                    op("dve", RECIP(rdc[:, :], PS[2][:, :]), reads=[BPS[2]], writes=[Brdc])
                    op("dve", TT(o2[:, j, :], PS[3][:, :], rdc[:, :], ALU.mult), reads=[BPS[3], Brdc], writes=[Bo2])
                    yield
                for fo in range(8):
                    pb = fo % 2
                    group("pe", [MM(PS[pb][:, :], wo[:, j, fo * 128:(fo + 1) * 128], o2[:, j, :], start=(j == 0), stop=(j == 3)) for j in range(4)],
                          reads=[Bwo, Bo2], writes=[BPS[pb]])
                    op("dve", TT(x[:, fo, :], x[:, fo, :], PS[pb][:, :], ALU.add), reads=[BPS[pb], Bx[b]], writes=[Bx[b]])
                    yield

            def ffn(n, nxt):
                b = n % 2
                ts = slice(n * 512, (n + 1) * 512)
                x = xt[b]
                rmsnorm(x, Bx[b], PC.ffn + l * 8, hT[1], Bh[1], sq, Bsq, lnv, Blnv, rstd, Brstd, 6)
                h3 = hT[1]
                for j in range(22):
                    for kind, cidx in ((0, j), (1, j + 22)):
                        wi = cnts["w1"] % NW1
                        cnts["w1"] += 1
                        dma("sp", DMA(wch[wi][:, :, :], b_ffi[l][cidx]), writes=[Bwch[wi]])
                        pb = 4 + (cnts["uc"] % 3)
                        ub = cnts["uc"] % 3
                        cnts["uc"] += 1
                        group("pe", [MM(PS[pb][:, :], wch[wi][:, kc, :], h3[:, kc, :], start=(kc == 0), stop=(kc == 7)) for kc in range(8)],
                              reads=[Bwch[wi], Bh[1]], writes=[BPS[pb]])
                        u = U[ub]
                        op("pool", CP(u[:, 0:2], hist[:, cidx, :]), reads=[Bhist[cidx]], writes=[BU[ub]])
                        op("act", ACT(u[:, 2:514], PS[pb][:, :], AF.Copy), reads=[BPS[pb]], writes=[BU[ub]])
                        op("pool", CP(hist[:, cidx, :], u[:, 512:514]), reads=[BU[ub]], writes=[Bhist[cidx]])
                        cw = PC.convw + (l * 3) * NFC + cidx
                        cbias = PC.convb + l * NFC + cidx
                        k1 = cnts["uc"] % 2
                        op("act", ACT(c1[k1][:, :], PS[pb][:, :], AF.Identity, bias=prm[:, cbias:cbias + 1], scale=prm[:, cw + 2 * NFC:cw + 2 * NFC + 1]),
                           reads=[BPS[pb], B_prm], writes=[Bc1[k1]])
                        op("dve", STT(c1[k1][:, :], u[:, 1:513], prm[:, cw + NFC:cw + NFC + 1], c1[k1][:, :], ALU.mult, ALU.add),
                           reads=[BU[ub], Bc1[k1], B_prm], writes=[Bc1[k1]])
                        if kind == 0:
                            op("dve", STT(cg[:, :], u[:, 0:512], prm[:, cw:cw + 1], c1[k1][:, :], ALU.mult, ALU.add),
                               reads=[BU[ub], Bc1[k1], B_prm], writes=[Bcg])
                            op("act", ACT(sgl[:, :], cg[:, :], AF.Silu), reads=[Bcg], writes=[Bsgl])
                        else:
                            op("dve", STT(cu[:, :], u[:, 0:512], prm[:, cw:cw + 1], c1[k1][:, :], ALU.mult, ALU.add),
                               reads=[BU[ub], Bc1[k1], B_prm], writes=[Bcu])
                            op("dve", TT(aT[:, j, :], sgl[:, :], cu[:, :], ALU.mult), reads=[Bsgl, Bcu], writes=[BaT[j]])
                        next(nxt, None)
                for _ in nxt:
                    pass
                for half in range(2):
                    pbs = [[0, 1, 2, 3], [4, 5, 6, 3]][half]
                    for j in range(22):
                        wi = cnts["w2"] % NW2
                        cnts["w2"] += 1
                        dma("sp", DMA(wo2[wi][:, :], b_ffo[l][j * 128:(j + 1) * 128, half * 512:(half + 1) * 512]), writes=[Bwo2[wi]])
                        for q in range(4):
                            op("pe", MM(PS[pbs[q]][:, :], wo2[wi][:, q * 128:(q + 1) * 128], aT[:, j, :], start=(j == 0), stop=(j == 21)),
                               reads=[Bwo2[wi], BaT[j]], writes=[BPS[pbs[q]]], accum=(j > 0))
                    for q in range(4):
                        fo = half * 4 + q
                        op("dve", TT(x[:, fo, :], x[:, fo, :], PS[pbs[q]][:, :], ALU.add), reads=[BPS[pbs[q]], Bx[b]], writes=[Bx[b]])
                dma("sp", DMA(xdst[:, :, ts].rearrange("c p t -> p c t"), x[:]), reads=[Bx[b]])

            for _ in prologue(0):
                pass
            for n in range(NT):
                nxt = prologue(n + 1) if n + 1 < NT else iter(())
                ffn(n, nxt)

        if "W" in phases:
            phaseW()
            barrier()
        for l in range(L):
            if "A" in phases:
                phaseA(l)
                barrier()
            if l % 2 == 0:
                if "B" in phases:
                    phaseB_sb()
                    barrier()
                if "M" in phases:
                    phaseB_moba()
            else:
                if "S" in phases:
                    phaseB_swa(l // 2)
            barrier()
            if "C" in phases:
                phaseC(l, l == L - 1)
                barrier()
        S_.finalize()
    return nc, S_


def host_consts(S):
    bf = ml_dtypes.bfloat16
    cbm = np.zeros((128, 384 + 8 * 512), np.float32)
    cbm[:, 0:128] = np.eye(128)
    cbm[:, 128:256] = 1.0
    cbm[:, 256:384] = (np.arange(128)[:, None] >= np.arange(128)[None, :])
    jj = np.arange(128)[:, None]
    tq = np.arange(512)[None, :]
    for i in range(4):
        masked = (128 * i + jj) >= tq
        cbm[:, 384 + i * 512:384 + (i + 1) * 512] = np.where(masked, -2048.0, 0.0)
        cbm[:, 384 + (4 + i) * 512:384 + (5 + i) * 512] = np.where(masked, 256.0, 0.0)
    cfm = np.zeros((128, 256), np.float32)
    cfm[:, 0:128] = (np.arange(128)[:, None] >= np.arange(128)[None, :])
    cfm[:, 128:256] = 1.0
    blk = (np.arange(S)[None, :] // 256 == np.arange(32)[:, None]).astype(np.float32)
    NCh, NBh = S // 128, S // 256
    own = (np.arange(NCh) // 2)[:, None]
    bb = np.arange(NBh)[None, :]
    bm = np.where(bb == own, 1e30, np.where(bb > own, -1e30, 0.0)).astype(np.float32)
    bmask = np.ascontiguousarray(np.broadcast_to(bm.reshape(1, NCh * NBh), (128, NCh * NBh)))
    return cbm.astype(bf), cfm, blk.astype(bf), bmask


def host_layout(inputs, S, L, Bsz):
    f32 = np.float32
    PC = PCols(L)
    ne, no = (L + 1) // 2, L // 2
    g = {k: np.asarray(v, dtype=f32) for k, v in inputs.items()}
    prm = np.zeros((128, PC.n), f32)

    def fm(v):
        return v.reshape(L, 8, 128).transpose(2, 0, 1).reshape(128, L * 8)
    prm[:, PC.mixn:PC.mixn + L * 8] = fm(g["mix_norm"][:L])
    prm[:, PC.cxn:PC.cxn + L * 8] = fm(g["cx_norm"][:L])
    prm[:, PC.ffn:PC.ffn + L * 8] = fm(g["ff_norm"][:L])
    prm[:, PC.memn:PC.memn + L * 8] = fm(g["cx_mem_norm"][:L])
    prm[:, PC.cxqg:PC.cxqg + L] = g["cx_q_gain"][:L].T
    prm[:, PC.cxkg:PC.cxkg + L] = g["cx_k_gain"][:L].T
    prm[0:64, PC.evqg:PC.evqg + ne] = g["ev_q_gain"][:ne].T
    prm[0:64, PC.evkg:PC.evkg + ne] = g["ev_k_gain"][:ne].T
    if no:
        prm[0:64, PC.odqg:PC.odqg + no] = g["od_q_gain"][:no].T
        prm[0:64, PC.odkg:PC.odkg + no] = g["od_k_gain"][:no].T
        prm[:, PC.sink:PC.sink + no * 16] = np.broadcast_to(g["od_sinks"][:no].reshape(1, no * 16), (128, no * 16))
    cw = g["ff_conv_w"][:L].reshape(L, 3, NFC, 128).transpose(3, 0, 1, 2).reshape(128, L * 3 * NFC)
    prm[:, PC.convw:PC.convw + L * 3 * NFC] = cw
    prm[:, PC.convb:PC.convb + L * NFC] = g["ff_conv_b"][:L].reshape(L, NFC, 128).transpose(2, 0, 1).reshape(128, L * NFC)
    rb = g["rel_bias"]
    WT = S + 384
    p = np.arange(128)[:, None]
    c = np.arange(WT)[None, :] - 384
    dist = c - p
    bidx = t5_bucket_np(dist)
    toe_moba = np.empty((8, 128, WT), f32)
    for hm in range(8):
        toe_moba[hm] = np.where(dist >= 0, rb[bidx, 8 + hm], f32(-30000.0))
    tqq = np.arange(128)[None, :]
    toe_swa = np.empty((16, 128, 256), f32)
    for which in range(2):
        d2 = (128 if which == 0 else 0) + tqq - p
        ok = (d2 >= 0) & (d2 < 128)
        b2 = t5_bucket_np(d2)
        for hh in range(16):
            toe_swa[hh, :, which * 128:(which + 1) * 128] = np.where(ok, rb[b2, hh], f32(-30000.0))
    cbm, cfm, blk, bmask = host_consts(S)
    shared = {
        "params": prm, "toe_moba": toe_moba, "toe_swa": toe_swa, "cb": cbm, "cf": cfm, "blkoh": blk, "bmask": bmask,
        "ev_w_in": g["ev_w_in"][:ne], "ev_w_out": g["ev_w_out"][:ne],
        "od_w_in": (g["od_w_in"][:no] if no else np.zeros((1, D, 1280), f32)),
        "od_w_out": (g["od_w_out"][:no] if no else np.zeros((1, D, D), f32)),
        "cx_w_q": g["cx_w_q"][:L], "cx_w_kv": g["cx_w_kv"][:L], "cx_w_o": g["cx_w_o"][:L],
        "ff_w_in": g["ff_w_in"][:L], "ff_w_out": g["ff_w_out"][:L],
    }
    shared = {k: np.ascontiguousarray(v) for k, v in shared.items()}
    maps = []
    for b in range(Bsz):
        m = dict(shared)
        m["xT"] = np.ascontiguousarray(g["x"][b].T.reshape(8, 128, S))
        m["memT"] = np.ascontiguousarray(g["mem"][b].T.reshape(8, 128, MEM))
        maps.append(m)
    return maps


_CACHE = {}
DBG = {}


def run(inputs, S, L, Bsz, n_cores, sb_lookback=None, phases="WABMSC"):
    key = (S, L, sb_lookback, phases)
    if key not in _CACHE:
        _CACHE[key] = build_program(S, L, sb_lookback, phases)
    nc, _ = _CACHE[key]
    maps = host_layout(inputs, S, L, Bsz)
    active = [0, 1, 4, 5][:Bsz] if n_cores == 8 else list(range(min(Bsz, n_cores)))
    zero_map = None
    in_maps = []
    for c in range(n_cores):
        if c in active:
            in_maps.append(maps[active.index(c)])
        else:
            if zero_map is None:
                zero_map = {k: (v if k in ("cb", "cf", "blkoh", "bmask") else np.zeros_like(v)) for k, v in maps[0].items()}
            in_maps.append(zero_map)
    if DBG.get("trace"):
        res = run_bass_kernel_spmd(nc, in_maps, core_ids=list(range(n_cores)), trace=True)
        DBG["exec_ns"] = res.exec_time_ns
    else:
        res = run_bass_kernel_spmd(nc, in_maps, core_ids=list(range(n_cores)))
    DBG["last"] = res.results
    outs = []
    for b in range(Bsz):
        yT = np.asarray(res.results[active[b]]["yT"], dtype=np.float32).reshape(1024, S)
        outs.append(yT.T)
    return np.ascontiguousarray(np.stack(outs, 0))


def kernel(**inputs):
    x = np.asarray(inputs["x"])
    Bsz, S, _ = x.shape
    L = int(np.asarray(inputs["mix_norm"]).shape[0])
    return run(inputs, S, L, Bsz, 8)
```
